# Optimizing a Trainium2 kernel written in Bass

```python
import math
import jax, jax.numpy as jnp
from jax import lax
import numpy as np

D_MODEL = 1024
BATCH = 8
SEQ = 4096
DEPTH = 1

MIX_WIDTH = D_MODEL
HEAD_DIM = 64
GMLP_GROUPS = 8
GMLP_WIDTH = GMLP_GROUPS * HEAD_DIM
CHUNK = 128
N_Q_HEADS = 8
N_KV_HEADS = 2
GQA_GROUP = N_Q_HEADS // N_KV_HEADS
ATTN_WIDTH = N_Q_HEADS * HEAD_DIM
KV_WIDTH = N_KV_HEADS * HEAD_DIM
WINDOW = 128
Q_BLOCK = 128
ROPE_THETA = 500000.0
ROT_DIM = HEAD_DIM // 4
D_FF = 4 * D_MODEL
IN_PROJ_WIDTH = 2 * GMLP_WIDTH + ATTN_WIDTH + 2 * KV_WIDTH
N_MOD = 6
EPS = 1e-5

kernel_name = "hybrid_gmlp_swa_sink_block"


def rms_norm(x, g):
    xf = x.astype(jnp.float32)
    y = xf * lax.rsqrt(jnp.mean(xf * xf, axis=-1, keepdims=True) + EPS)
    return (y * g.astype(jnp.float32)).astype(x.dtype)


def modulate(h, shift, scale):
    return h * (1 + scale[:, None, :]) + shift[:, None, :]


def partial_rope(t, positions):
    half = ROT_DIM // 2
    inv_freq = ROPE_THETA ** (-jnp.arange(0, ROT_DIM, 2, dtype=jnp.float32) / ROT_DIM)
    ang = positions.astype(jnp.float32)[..., None] * inv_freq
    cos = jnp.cos(ang)[:, :, None, :].astype(t.dtype)
    sin = jnp.sin(ang)[:, :, None, :].astype(t.dtype)
    t1 = t[..., :half]
    t2 = t[..., half:ROT_DIM]
    return jnp.concatenate([t1 * cos - t2 * sin, t2 * cos + t1 * sin, t[..., ROT_DIM:]], axis=-1)


def chunked_sgu(z, w_s, b_s):
    B, S, _ = z.shape
    n_chunks = S // CHUNK
    u, v = jnp.split(z, 2, axis=-1)
    v = v.reshape(B, n_chunks, CHUNK, GMLP_GROUPS, HEAD_DIM)
    causal = jnp.tril(jnp.ones((CHUNK, CHUNK), dtype=w_s.dtype))
    w = w_s * causal[None]
    sv = jnp.einsum('hts,bcshd->bcthd', w, v) + jnp.transpose(b_s)[None, None, :, :, None]
    return u * sv.reshape(B, S, GMLP_WIDTH)


def sliding_window_sink_attention(q, k, v, sinks, positions):
    B, S, _, _ = q.shape
    nb = S // Q_BLOCK
    q = partial_rope(q, positions)
    k = partial_rope(k, positions)
    qb = q.reshape(B, nb, Q_BLOCK, N_KV_HEADS, GQA_GROUP, HEAD_DIM)
    pad = jnp.zeros((B, Q_BLOCK, N_KV_HEADS, HEAD_DIM), k.dtype)
    kp = jnp.concatenate([pad, k], axis=1)
    vp = jnp.concatenate([pad, v], axis=1)
    kb = jnp.concatenate([kp[:, :S].reshape(B, nb, Q_BLOCK, N_KV_HEADS, HEAD_DIM),
                          k.reshape(B, nb, Q_BLOCK, N_KV_HEADS, HEAD_DIM)], axis=2)
    vb = jnp.concatenate([vp[:, :S].reshape(B, nb, Q_BLOCK, N_KV_HEADS, HEAD_DIM),
                          v.reshape(B, nb, Q_BLOCK, N_KV_HEADS, HEAD_DIM)], axis=2)
    scores = jnp.einsum('bnqhgd,bnkhd->bnhgqk', qb, kb).astype(jnp.float32) / math.sqrt(HEAD_DIM)
    qi = jnp.arange(Q_BLOCK)[:, None]
    kj = jnp.arange(2 * Q_BLOCK)[None, :]
    band = (kj > qi) & (kj <= qi + WINDOW)
    valid_first = jnp.arange(nb)[:, None, None] > 0
    mask = band[None] & (valid_first | (kj >= Q_BLOCK)[None])
    mask = mask[None, :, None, None]
    scores = jnp.where(mask, scores, -jnp.inf)
    sink = sinks.astype(jnp.float32).reshape(N_KV_HEADS, GQA_GROUP)[None, None, :, :, None, None]
    m = jnp.maximum(jnp.max(scores, axis=-1, keepdims=True), sink)
    p = jnp.exp(scores - m)
    denom = jnp.sum(p, axis=-1, keepdims=True) + jnp.exp(sink - m)
    probs = (p / denom).astype(v.dtype)
    out = jnp.einsum('bnhgqk,bnkhd->bnqhgd', probs, vb)
    return out.reshape(B, S, ATTN_WIDTH)


def setup_inputs(seed: int = 0) -> dict:
    key = jax.random.key(seed)
    ks = jax.random.split(key, 16)
    f32 = jnp.float32
    x = jax.random.normal(ks[0], (BATCH, SEQ, D_MODEL), f32)
    c = jax.random.normal(ks[1], (BATCH, D_MODEL), f32)
    offsets = jax.random.randint(ks[2], (BATCH, 1), 0, 2048, dtype=jnp.int32)
    positions = offsets + jnp.arange(SEQ, dtype=jnp.int32)[None, :]
    w_ada = jax.random.normal(ks[3], (DEPTH, D_MODEL, N_MOD * D_MODEL), f32) * (0.5 * D_MODEL ** -0.5)
    b_ada = jax.random.normal(ks[4], (DEPTH, N_MOD * D_MODEL), f32) * 0.02
    g_mix = 1.0 + 0.02 * jax.random.normal(ks[5], (DEPTH, D_MODEL), f32)
    w_in = jax.random.normal(ks[6], (DEPTH, D_MODEL, IN_PROJ_WIDTH), f32) * D_MODEL ** -0.5
    w_spatial = jax.random.normal(ks[7], (DEPTH, GMLP_GROUPS, CHUNK, CHUNK), f32) * CHUNK ** -0.5
    b_spatial = 1.0 + 0.01 * jax.random.normal(ks[8], (DEPTH, GMLP_GROUPS, CHUNK), f32)
    sinks = jax.random.normal(ks[9], (DEPTH, N_Q_HEADS), f32) * 0.5
    w_out = jax.random.normal(ks[10], (DEPTH, MIX_WIDTH, D_MODEL), f32) * MIX_WIDTH ** -0.5
    g_ffn = 1.0 + 0.02 * jax.random.normal(ks[11], (DEPTH, D_MODEL), f32)
    w_ff1 = jax.random.normal(ks[12], (DEPTH, D_MODEL, D_FF), f32) * D_MODEL ** -0.5
    w_ff2 = jax.random.normal(ks[13], (DEPTH, D_FF, D_MODEL), f32) * D_FF ** -0.5
    g_final = 1.0 + 0.02 * jax.random.normal(ks[14], (D_MODEL,), f32)
    return {"x": x, "c": c, "positions": positions, "w_ada": w_ada, "b_ada": b_ada,
            "g_mix": g_mix, "w_in": w_in, "w_spatial": w_spatial, "b_spatial": b_spatial,
            "sinks": sinks, "w_out": w_out, "g_ffn": g_ffn, "w_ff1": w_ff1, "w_ff2": w_ff2,
            "g_final": g_final}


def reference(x, c, positions, w_ada, b_ada, g_mix, w_in, w_spatial, b_spatial, sinks,
              w_out, g_ffn, w_ff1, w_ff2, g_final):
    B, S, _ = x.shape
    c_act = jax.nn.silu(c)
    for l in range(DEPTH):
        mod = c_act @ w_ada[l] + b_ada[l]
        shift1, scale1, gate1, shift2, scale2, gate2 = jnp.split(mod, N_MOD, axis=-1)

        h = modulate(rms_norm(x, g_mix[l]), shift1, scale1)
        proj = h @ w_in[l]
        z_a = proj[..., :2 * GMLP_WIDTH]
        o = 2 * GMLP_WIDTH
        q = proj[..., o:o + ATTN_WIDTH].reshape(B, S, N_Q_HEADS, HEAD_DIM)
        o += ATTN_WIDTH
        k = proj[..., o:o + KV_WIDTH].reshape(B, S, N_KV_HEADS, HEAD_DIM)
        o += KV_WIDTH
        v = proj[..., o:o + KV_WIDTH].reshape(B, S, N_KV_HEADS, HEAD_DIM)

        out_a = chunked_sgu(jax.nn.gelu(z_a), w_spatial[l], b_spatial[l])
        out_b = sliding_window_sink_attention(q, k, v, sinks[l], positions)
        mix = jnp.concatenate([out_a, out_b], axis=-1) @ w_out[l]
        x = x + gate1[:, None, :] * mix

        h2 = modulate(rms_norm(x, g_ffn[l]), shift2, scale2)
        ff = jnp.square(jax.nn.relu(h2 @ w_ff1[l])) @ w_ff2[l]
        x = x + gate2[:, None, :] * ff
    return rms_norm(x, g_final)
```

```python
import math
from contextlib import ExitStack

import numpy as np
import concourse.bass as bass
import concourse.mybir as mybir
from concourse.bass_utils import run_bass_kernel_spmd

F32 = mybir.dt.float32
BF16 = mybir.dt.bfloat16
I32 = mybir.dt.int32
AF = mybir.ActivationFunctionType
ALU = mybir.AluOpType

D = 1024
DFF = 4096
INW = 1792
NMOD = 6
EPS = 1e-5
ROPE_THETA = 500000.0
N_CORES = 8


class Tok:
    __slots__ = ("sem", "key", "val")

    def __init__(self, sem, key, val):
        self.sem, self.key, self.val = sem, key, val


class Buf:
    def __init__(self, name):
        self.name = name
        self.w = None
        self.r = []


class DmaSem:
    def __init__(self, sem, key):
        self.sem, self.key, self.val = sem, key, 0


class Prog:
    def __init__(self, nc, es):
        self.nc = nc
        self.es = es
        self.eng = {"pe": nc.tensor, "act": nc.scalar, "dve": nc.vector, "pool": nc.gpsimd, "sp": nc.sync}
        self.esem = {}
        self.ecnt = {}
        self.waited = {}
        for k in self.eng:
            self.esem[k] = es.enter_context(nc.semaphore("es_" + k))
            self.ecnt[k] = 0
            self.waited[k] = {}
        self.dsems = []
        self.nds = 0

    def dma_sem(self):
        self.nds += 1
        key = "ds%d" % self.nds
        d = DmaSem(self.es.enter_context(self.nc.semaphore(key)), key)
        self.dsems.append(d)
        return d

    def _deps(self, reads, writes):
        toks = []
        for b in reads:
            if b.w is not None:
                toks.append(b.w)
        for b in writes:
            if b.w is not None:
                toks.append(b.w)
            toks.extend(b.r)
        return toks

    def _wait(self, e, toks):
        need = {}
        for t in toks:
            if t.key not in need or need[t.key].val < t.val:
                need[t.key] = t
        for key, t in need.items():
            if self.waited[e].get(key, 0) >= t.val:
                continue
            self.eng[e].wait_ge(t.sem, t.val)
            self.waited[e][key] = t.val

    def _record(self, tok, reads, writes):
        for b in reads:
            b.r.append(tok)
        for b in writes:
            b.w = tok
            b.r = []

    def op(self, e, fn, reads=(), writes=()):
        self._wait(e, self._deps(reads, writes))
        ins = fn(self.eng[e])
        self.ecnt[e] += 1
        ins.then_inc(self.esem[e], 1)
        tok = Tok(self.esem[e], "e_" + e, self.ecnt[e])
        self._record(tok, reads, writes)
        return tok

    def dma(self, e, ds, out, in_, reads=(), writes=(), record=True, **kw):
        self._wait(e, self._deps(reads, writes))
        self.eng[e].dma_start(out=out, in_=in_, **kw).then_inc(ds.sem, 16)
        ds.val += 16
        tok = Tok(ds.sem, ds.key, ds.val)
        if record:
            self._record(tok, reads, writes)
        return tok

    def barrier(self):
        toks = [Tok(self.esem[k], "e_" + k, self.ecnt[k]) for k in self.eng if self.ecnt[k] > 0]
        toks += [Tok(d.sem, d.key, d.val) for d in self.dsems if d.val > 0]
        for e in self.eng:
            self._wait(e, toks)

    def finish(self, toks):
        self._wait("sp", toks)


def _cw_consts():
    two_pi = 2.0 * math.pi
    c1 = 6.28125
    r = np.array([two_pi - c1], dtype=np.float32)
    c2 = float((r.view(np.uint32) & np.uint32(0xFFFFF000)).view(np.float32)[0])
    c3 = float(np.float32(two_pi - c1 - c2))
    return c1, c2, c3


def _build(NT):
    assert NT % 4 == 0
    S = NT * 128
    NS = NT // 4
    nc = bass.Bass("TRN2", target_bir_lowering=False)
    dt = nc.dram_tensor
    x_d = dt("x", [S, D], F32, kind="ExternalInput").ap()
    c_d = dt("c", [8, 128], F32, kind="ExternalInput").ap()
    pos_d = dt("positions", [NT, 128], I32, kind="ExternalInput").ap()
    wada_d = dt("w_ada", [D, NMOD * D], F32, kind="ExternalInput").ap()
    bada_d = dt("b_ada", [1, NMOD * D], F32, kind="ExternalInput").ap()
    gmix_d = dt("g_mix", [8, 128], F32, kind="ExternalInput").ap()
    win_d = dt("w_in", [D, INW], F32, kind="ExternalInput").ap()
    wsp_d = dt("w_spatial", [8, 128, 128], F32, kind="ExternalInput").ap()
    bsp_d = dt("b_spatial", [8, 128], F32, kind="ExternalInput").ap()
    snk_d = dt("sinks", [8], F32, kind="ExternalInput").ap()
    wout_d = dt("w_out", [D, D], F32, kind="ExternalInput").ap()
    gffn_d = dt("g_ffn", [8, 128], F32, kind="ExternalInput").ap()
    wff1_d = dt("w_ff1", [D, DFF], F32, kind="ExternalInput").ap()
    wff2_d = dt("w_ff2", [DFF, D], F32, kind="ExternalInput").ap()
    gfin_d = dt("g_final", [D], F32, kind="ExternalInput").ap()
    y_d = dt("y", [S, D], F32, kind="ExternalOutput").ap()
    x1_d = dt("x1s", [S, D], F32, kind="Internal").ap()

    with ExitStack() as es:
        P = Prog(nc, es)
        sb = lambda name, shape, dtype: es.enter_context(nc.sbuf_tensor(name, shape, dtype))
        ps = es.enter_context(nc.psum_tensor("ps", [128, 4096], F32))
        PB = [Buf("pb%d" % i) for i in range(8)]

        def bank(i, w=512, off=0):
            return ps[:, i * 512 + off: i * 512 + off + w]

        def bank_bf(i, parts=128):
            return ps[0:parts, i * 512:(i + 1) * 512].bitcast(BF16)

        w_ff1 = sb("w_ff1b", [128, 8, DFF], BF16)
        identf = sb("identf", [128, 128], F32)
        identb = sb("identb", [128, 128], BF16)
        gfin_bc = sb("gfin_bc", [128, D], F32)
        gate2_bc = sb("gate2_bc", [128, D], F32)
        modcol = sb("modcol", [128, 48], F32)
        Gc = sb("Gc", [128, 16], F32)
        ones_row = sb("ones_row", [1, 128], F32)
        negh = sb("negh", [128, 1], F32)
        ssq = sb("ssq", [128, 4], F32)
        msq = sb("msq", [128, 4], F32)
        rstd = sb("rstd", [128, 4], F32)
        B_wff1, B_identf, B_identb, B_gfin, B_gate2 = (Buf(n) for n in ("wff1", "identf", "identb", "gfin", "gate2"))
        B_modcol, B_Gc, B_ones, B_negh = (Buf(n) for n in ("modcol", "Gc", "ones", "negh"))
        B_ssq = [Buf("ssq%d" % i) for i in range(4)]
        B_msq = [Buf("msq%d" % i) for i in range(4)]
        B_rstd = [Buf("rstd%d" % i) for i in range(4)]

        P.op("pool", lambda g: g.memset(identf[:], 0.0), writes=[B_identf])
        P.op("pool", lambda g: g.affine_select(out=identf[:], in_=identf[:], pattern=[[1, 128]], base=0,
                                                channel_multiplier=-1, compare_op=ALU.not_equal, fill=1.0),
             reads=[B_identf], writes=[B_identf])
        P.op("dve", lambda v: v.tensor_copy(out=identb[:], in_=identf[:]), reads=[B_identf], writes=[B_identb])
        P.op("pool", lambda g: g.memset(ones_row[:], 1.0), writes=[B_ones])
        P.op("pool", lambda g: g.memset(negh[:], -0.5), writes=[B_negh])
        ds_small = P.dma_sem()
        P.dma("sp", ds_small, gfin_bc[:], gfin_d.partition_broadcast(128), record=False)

        def rms_stats(src_ap, B_src, slot, junk_ap, B_junk):
            P.op("act", lambda a: a.activation(out=junk_ap, in_=src_ap, func=AF.Square,
                                               accum_out=ssq[:, slot:slot + 1]),
                 reads=[B_src], writes=[B_junk, B_ssq[slot]])
            P.op("pool", lambda g: g.tensor_scalar(out=msq[:, slot:slot + 1], in0=ssq[:, slot:slot + 1],
                                                   scalar1=1.0 / D, scalar2=EPS, op0=ALU.mult, op1=ALU.add),
                 reads=[B_ssq[slot]], writes=[B_msq[slot]])
            P.op("pool", lambda g: g.tensor_tensor(out=rstd[:, slot:slot + 1], in0=msq[:, slot:slot + 1],
                                                   in1=negh[:], op=ALU.pow),
                 reads=[B_msq[slot], B_negh], writes=[B_rstd[slot]])
            return B_rstd[slot]

        with ExitStack() as es1:
            sb1 = lambda name, shape, dtype: es1.enter_context(nc.sbuf_tensor(name, shape, dtype))
            w_in = sb1("w_inb", [128, 8, INW], BF16)
            w_out = sb1("w_outb", [128, 8, D], BF16)
            stage = [sb1("stage%d" % i, [128, 8, 256], F32) for i in range(2)]
            brow = [sb1("brow%d" % i, [1, 256], F32) for i in range(2)]
            mrow = [sb1("mrow%d" % i, [1, 256], F32) for i in range(2)]
            gate1_bc = sb1("gate1_bc", [128, D], F32)
            c8 = sb1("c8", [16, 128], F32)
            g16 = sb1("g16", [16, 128], F32)
            cact_col = sb1("cact_col", [128, 8], F32)
            WsT = sb1("WsT", [128, 8, 128], BF16)
            wsf = sb1("wsf", [128, 8, 128], F32)
            bsp8 = sb1("bsp8", [8, 128], F32)
            bcol = sb1("bcol", [128, 8], F32)
            sk = sb1("sk", [128, 8], F32)
            esink = sb1("esink", [128, 8], F32)
            mask_prev = sb1("mask_prev", [128, 128], BF16)
            mask_cur = sb1("mask_cur", [128, 128], BF16)
            posi = sb1("posi", [NT, 128], I32)
            posf = sb1("posf", [NT, 128], F32)
            posT = sb1("posT", [128, NT], F32)
            ang = sb1("ang", [128, NT, 8], F32)
            tq = sb1("tq", [128, NT, 8], F32)
            ki = sb1("ki", [128, NT, 8], I32)
            kf = sb1("kf", [128, NT, 8], F32)
            rr = sb1("rr", [128, NT, 8], F32)
            rc = sb1("rc", [128, NT, 8], F32)
            cos_t = sb1("cos_t", [128, NT, 8], F32)
            sin_t = sb1("sin_t", [128, NT, 8], F32)
            xb = [sb1("xb%d" % i, [128, D], F32) for i in range(3)]
            junk = sb1("junk", [128, D], BF16)
            xn = [sb1("xn%d" % i, [128, D], BF16) for i in range(2)]
            hT = [sb1("hT%d" % i, [128, 8, 128], BF16) for i in range(2)]
            uv = [sb1("uv%d" % i, [128, D], BF16) for i in range(2)]
            qkr = [sb1("qkr%d" % i, [128, 10, 64], BF16) for i in range(2)]
            rt = [sb1("rt%d" % i, [128, 10, 8], F32) for i in range(4)]
            Vaug = [sb1("Vaug%d" % i, [128, 2, 65], BF16) for i in range(2)]
            KT = [sb1("KT%d" % i, [64, 2, 128], BF16) for i in range(2)]
            QT = sb1("QT", [64, 8, 128], BF16)
            PT = [sb1("PT%d" % i, [128, 512], BF16) for i in range(4)]
            tmp_sv = sb1("tmp_sv", [128, 512], F32)
            den = sb1("den", [128, 8], F32)
            rden = sb1("rden", [128, 8], F32)
            mix = [sb1("mix%d" % i, [128, D], BF16) for i in range(2)]
            mixT = [sb1("mixT%d" % i, [128, 8, 128], BF16) for i in range(2)]

            B_win, B_wout, B_gate1, B_c8, B_g16, B_cact = (Buf(n) for n in ("win", "wout", "gate1", "c8", "g16", "cact"))
            B_stage = [Buf("stage%d" % i) for i in range(2)]
            B_brow = [Buf("brow%d" % i) for i in range(2)]
            B_mrow = [Buf("mrow%d" % i) for i in range(2)]
            B_WsT, B_wsf, B_bsp8, B_bcol, B_sk, B_esink = (Buf(n) for n in ("WsT", "wsf", "bsp8", "bcol", "sk", "esink"))
            B_mprev, B_mcur = Buf("mprev"), Buf("mcur")
            B_pos = Buf("pos")
            B_cs = Buf("cossin")
            B_xb = [Buf("xb%d" % i) for i in range(3)]
            B_junk = Buf("junk")
            B_xn = [Buf("xn%d" % i) for i in range(2)]
            B_hT = [Buf("hT%d" % i) for i in range(2)]
            B_uv = [Buf("uv%d" % i) for i in range(2)]
            B_qkr = [Buf("qkr%d" % i) for i in range(2)]
            B_rt = [Buf("rt%d" % i) for i in range(4)]
            B_Vaug = [Buf("Vaug%d" % i) for i in range(2)]
            B_KT = [Buf("KT%d" % i) for i in range(2)]
            B_QT = Buf("QT")
            B_PT = [Buf("PT%d" % i) for i in range(4)]
            B_tmpsv, B_den, B_rden = Buf("tmpsv"), Buf("den"), Buf("rden")
            B_mix = [Buf("mix%d" % i) for i in range(2)]
            B_mixT = [Buf("mixT%d" % i) for i in range(2)]

            P.dma("sp", ds_small, c8[0:8, :], c_d, record=False)
            P.dma("sp", ds_small, g16[0:8, :], gmix_d, record=False)
            P.dma("sp", ds_small, g16[8:16, :], gffn_d, record=False)
            P.dma("sp", ds_small, bsp8[:], bsp_d, record=False)
            P.dma("sp", ds_small, sk[:], snk_d.partition_broadcast(128), record=False)
            P.dma("sp", ds_small, posi[:], pos_d, record=False)
            t_small = P.dma("sp", ds_small, wsf[:], wsp_d.rearrange("g t s -> t g s"), record=False)
            for b_ in (B_gfin, B_c8, B_g16, B_bsp8, B_sk, B_pos, B_wsf):
                b_.w = t_small
            ds_x = [P.dma_sem() for _ in range(3)]
            for n in range(min(2, NT)):
                P.dma("sp", ds_x[n % 3], xb[n % 3][:], x_d[n * 128:(n + 1) * 128, :], writes=[B_xb[n % 3]])

            ds_win = P.dma_sem()
            win_v = win_d.rearrange("(k p) n -> p k n", p=128)
            for k in range(8):
                t = P.dma("pool", ds_win, w_in[:, k, :], win_v[:, k, :], record=False)
            B_win.w = t

            P.op("act", lambda a: a.activation(out=c8[0:8, :], in_=c8[0:8, :], func=AF.Silu),
                 reads=[B_c8], writes=[B_c8])
            P.op("pe", lambda t_: t_.transpose(out=bank(0, 8)[:, :], in_=c8[0:8, :], identity=identf[0:8, 0:8]),
                 reads=[B_c8, B_identf], writes=[PB[0]])
            P.op("dve", lambda v: v.tensor_copy(out=cact_col[:], in_=bank(0, 8)), reads=[PB[0]], writes=[B_cact])
            P.op("pe", lambda t_: t_.transpose(out=bank(0, 16)[:, :], in_=g16[0:16, :], identity=identf[0:16, 0:16]),
                 reads=[B_g16, B_identf], writes=[PB[0]])
            P.op("dve", lambda v: v.tensor_copy(out=modcol[:, 32:48], in_=bank(0, 16)), reads=[PB[0]], writes=[B_modcol])

            ds_st = [P.dma_sem() for _ in range(2)]
            ds_br = [P.dma_sem() for _ in range(2)]
            wada_v = wada_d.rearrange("(k p) n -> p k n", p=128)
            NCH = NMOD * D // 256
            for j in range(NCH):
                sidx = j % 2
                P.dma("sp", ds_st[sidx], stage[sidx][:], wada_v[:, :, j * 256:(j + 1) * 256], writes=[B_stage[sidx]])
                P.dma("sp", ds_br[sidx], brow[sidx][:], bada_d[0:1, j * 256:(j + 1) * 256], writes=[B_brow[sidx]])

                def mm(t_, sidx=sidx):
                    for k in range(8):
                        ins = t_.matmul(bank(1, 256)[0:1, :], lhsT=cact_col[:, k:k + 1], rhs=stage[sidx][:, k, :],
                                        start=(k == 0), stop=(k == 7))
                    return ins
                P.op("pe", mm, reads=[B_cact, B_stage[sidx]], writes=[PB[1]])
                P.op("dve", lambda v, sidx=sidx: v.tensor_tensor(out=mrow[sidx][:], in0=bank(1, 256)[0:1, :],
                                                                 in1=brow[sidx][:], op=ALU.add),
                     reads=[PB[1], B_brow[sidx]], writes=[B_mrow[sidx]])
                sec = (j * 256) // D
                off = (j * 256) % D
                if sec in (2, 5):
                    gdst = gate1_bc if sec == 2 else gate2_bc
                    Bg = B_gate1 if sec == 2 else B_gate2
                    P.op("pe", lambda t_, sidx=sidx: t_.matmul(bank(2, 256), lhsT=ones_row[0:1, :], rhs=mrow[sidx][0:1, :],
                                                                start=True, stop=True),
                         reads=[B_ones, B_mrow[sidx]], writes=[PB[2]])
                    P.op("act", lambda a, gdst=gdst, off=off: a.activation(out=gdst[:, off:off + 256], in_=bank(2, 256),
                                                                            func=AF.Copy),
                         reads=[PB[2]], writes=[Bg])
                else:
                    cbase = {0: 0, 1: 8, 3: 16, 4: 24}[sec] + off // 128

                    def tcol(t_, sidx=sidx):
                        for q in range(2):
                            ins = t_.matmul(bank(3, 2)[:, q:q + 1], lhsT=mrow[sidx][0:1, q * 128:(q + 1) * 128],
                                            rhs=ones_row[0:1, 0:1], start=True, stop=True)
                        return ins
                    P.op("pe", tcol, reads=[B_ones, B_mrow[sidx]], writes=[PB[3]])
                    P.op("dve", lambda v, cbase=cbase: v.tensor_copy(out=modcol[:, cbase:cbase + 2], in_=bank(3, 2)),
                         reads=[PB[3]], writes=[B_modcol])
            P.op("dve", lambda v: v.scalar_tensor_tensor(out=Gc[:, 0:8], in0=modcol[:, 8:16], scalar=1.0,
                                                         in1=modcol[:, 32:40], op0=ALU.add, op1=ALU.mult),
                 reads=[B_modcol], writes=[B_Gc])
            P.op("dve", lambda v: v.scalar_tensor_tensor(out=Gc[:, 8:16], in0=modcol[:, 24:32], scalar=1.0,
                                                         in1=modcol[:, 40:48], op0=ALU.add, op1=ALU.mult),
                 reads=[B_modcol, B_Gc], writes=[B_Gc])

            wout_v = wout_d.rearrange("(k p) n -> p k n", p=128)
            for k in range(8):
                sidx = k % 2
                st_v = stage[sidx][:, 0:4, :].rearrange("p a b -> p (a b)")
                P.dma("sp", ds_st[sidx], st_v, wout_v[:, k, :], writes=[B_stage[sidx]])
                P.op("pool", lambda g, k=k, st_v=st_v: g.tensor_tensor(out=w_out[:, k, :], in0=st_v, in1=gate1_bc[:],
                                                                        op=ALU.mult),
                     reads=[B_stage[sidx], B_gate1], writes=[B_wout])
            ds_wff1 = P.dma_sem()
            wff1_v = wff1_d.rearrange("(k p) n -> p k n", p=128)
            for k in range(8):
                for hh in range(2):
                    t = P.dma("pool", ds_wff1, w_ff1[:, k, hh * 2048:(hh + 1) * 2048],
                              wff1_v[:, k, hh * 2048:(hh + 1) * 2048], record=False)
            B_wff1.w = t

            P.op("pool", lambda g: g.affine_select(out=wsf[:], in_=wsf[:], pattern=[[0, 8], [-1, 128]], base=0,
                                                   channel_multiplier=1, compare_op=ALU.is_ge, fill=0.0),
                 reads=[B_wsf], writes=[B_wsf])
            for half in range(2):
                def tw(t_, half=half):
                    for gg in range(4):
                        g_ = half * 4 + gg
                        ins = t_.transpose(out=bank(4 + half)[:, gg * 128:(gg + 1) * 128], in_=wsf[:, g_, :],
                                           identity=identf[:])
                    return ins
                P.op("pe", tw, reads=[B_wsf, B_identf], writes=[PB[4 + half]])
                P.op("act", lambda a, half=half: a.activation(
                    out=WsT[:, half * 4:(half + 1) * 4, :].rearrange("p a b -> p (a b)"), in_=bank(4 + half), func=AF.Copy),
                    reads=[PB[4 + half]], writes=[B_WsT])
            P.op("pe", lambda t_: t_.transpose(out=bank(6, 8)[:, :], in_=bsp8[0:8, :], identity=identf[0:8, 0:8]),
                 reads=[B_bsp8, B_identf], writes=[PB[6]])
            P.op("dve", lambda v: v.tensor_copy(out=bcol[:], in_=bank(6, 8)), reads=[PB[6]], writes=[B_bcol])
            P.op("act", lambda a: a.activation(out=esink[:], in_=sk[:], func=AF.Exp), reads=[B_sk], writes=[B_esink])

            P.op("pool", lambda g: g.memset(mask_prev[:], 1.0), writes=[B_mprev])
            P.op("pool", lambda g: g.affine_select(out=mask_prev[:], in_=mask_prev[:], pattern=[[-1, 128]], base=-1,
                                                   channel_multiplier=1, compare_op=ALU.is_ge, fill=0.0),
                 reads=[B_mprev], writes=[B_mprev])
            P.op("pool", lambda g: g.memset(mask_cur[:], 1.0), writes=[B_mcur])
            P.op("pool", lambda g: g.affine_select(out=mask_cur[:], in_=mask_cur[:], pattern=[[1, 128]], base=0,
                                                   channel_multiplier=-1, compare_op=ALU.is_ge, fill=0.0),
                 reads=[B_mcur], writes=[B_mcur])
            for i in range(2):
                P.op("pool", lambda g, i=i: g.memset(Vaug[i][:, :, 64:65], 1.0), writes=[B_Vaug[i]])

            P.op("dve", lambda v: v.tensor_copy(out=posf[:], in_=posi[:]), reads=[B_pos], writes=[B_pos])
            P.op("pe", lambda t_: t_.transpose(out=bank(7, NT)[:, :], in_=posf[0:NT, :], identity=identf[0:NT, 0:NT]),
                 reads=[B_pos, B_identf], writes=[PB[7]])
            P.op("dve", lambda v: v.tensor_copy(out=posT[:], in_=bank(7, NT)), reads=[PB[7]], writes=[B_pos])
            for jf in range(8):
                invf = float(ROPE_THETA ** (-(2.0 * jf) / 16.0))
                P.op("dve", lambda v, jf=jf, invf=invf: v.tensor_scalar(out=ang[:, :, jf:jf + 1], in0=posT[:].unsqueeze(2),
                                                                         scalar1=float(np.float32(invf)), scalar2=None,
                                                                         op0=ALU.mult),
                     reads=[B_pos], writes=[B_pos])
            c1, c2, c3 = _cw_consts()
            PI_S = 3.1415925
            P.op("dve", lambda v: v.tensor_scalar(out=tq[:], in0=ang[:], scalar1=float(1.0 / (2.0 * math.pi)), scalar2=None,
                                                  op0=ALU.mult), reads=[B_pos], writes=[B_pos])
            P.op("dve", lambda v: v.tensor_copy(out=ki[:], in_=tq[:]), reads=[B_pos], writes=[B_pos])
            P.op("dve", lambda v: v.tensor_copy(out=kf[:], in_=ki[:]), reads=[B_pos], writes=[B_pos])
            P.op("dve", lambda v: v.scalar_tensor_tensor(out=rr[:], in0=kf[:], scalar=-c1, in1=ang[:], op0=ALU.mult,
                                                         op1=ALU.add), reads=[B_pos], writes=[B_pos])
            P.op("dve", lambda v: v.scalar_tensor_tensor(out=rc[:], in0=kf[:], scalar=-c2, in1=rr[:], op0=ALU.mult,
                                                         op1=ALU.add), reads=[B_pos], writes=[B_pos])
            P.op("dve", lambda v: v.scalar_tensor_tensor(out=rr[:], in0=kf[:], scalar=-c3, in1=rc[:], op0=ALU.mult,
                                                         op1=ALU.add), reads=[B_pos], writes=[B_pos])
            P.op("dve", lambda v: v.tensor_scalar(out=rc[:], in0=rr[:], scalar1=float(math.pi / 2.0), scalar2=None,
                                                  op0=ALU.add), reads=[B_pos], writes=[B_pos])
            P.op("dve", lambda v: v.tensor_scalar(out=tq[:], in0=rc[:], scalar1=float(math.pi), scalar2=float(-2.0 * math.pi),
                                                  op0=ALU.is_gt, op1=ALU.mult), reads=[B_pos], writes=[B_pos])
            P.op("dve", lambda v: v.tensor_tensor(out=rc[:], in0=rc[:], in1=tq[:], op=ALU.add), reads=[B_pos], writes=[B_pos])
            P.op("dve", lambda v: v.tensor_scalar(out=rr[:], in0=rr[:], scalar1=-PI_S, scalar2=PI_S, op0=ALU.max, op1=ALU.min),
                 reads=[B_pos], writes=[B_pos])
            P.op("dve", lambda v: v.tensor_scalar(out=rc[:], in0=rc[:], scalar1=-PI_S, scalar2=PI_S, op0=ALU.max, op1=ALU.min),
                 reads=[B_pos], writes=[B_pos])
            P.op("act", lambda a: a.activation(out=sin_t[:], in_=rr[:], func=AF.Sin), reads=[B_pos], writes=[B_cs])
            P.op("act", lambda a: a.activation(out=cos_t[:], in_=rc[:], func=AF.Sin), reads=[B_pos, B_cs], writes=[B_cs])

            ds_x1 = [P.dma_sem() for _ in range(3)]
            for n in range(NT):
                xi = n % 3
                pi = n % 2
                cur, prv = n % 2, (n - 1) % 2
                if n + 2 < NT:
                    nn = n + 2
                    P.dma("sp", ds_x[nn % 3], xb[nn % 3][:], x_d[nn * 128:(nn + 1) * 128, :], writes=[B_xb[nn % 3]])
                slot = n % 2
                rms_stats(xb[xi][:], B_xb[xi], slot, junk[:], B_junk)
                P.op("pool", lambda g: g.tensor_scalar(out=xn[pi][:], in0=xb[xi][:], scalar1=rstd[:, slot:slot + 1],
                                                       scalar2=1.0, op0=ALU.mult, op1=ALU.mult),
                     reads=[B_xb[xi], B_rstd[slot]], writes=[B_xn[pi]])
                b0 = bank_bf(0).rearrange("p (k t) -> p k t", k=8)

                def tx(t_):
                    for k in range(8):
                        ins = t_.transpose(out=b0[:, k, :], in_=xn[pi][:, k * 128:(k + 1) * 128], identity=identb[:])
                    return ins
                P.op("pe", tx, reads=[B_xn[pi], B_identb], writes=[PB[0]])
                for k in range(8):
                    P.op("act", lambda a, k=k: a.activation(out=hT[pi][:, k, :], in_=b0[:, k, :], func=AF.Identity,
                                                            scale=Gc[:, k:k + 1], bias=modcol[:, k:k + 1]),
                         reads=[PB[0], B_Gc, B_modcol], writes=[B_hT[pi]])
                gcols = [(0, 512), (512, 512), (1024, 512), (1536, 256)]
                for gi, (c0, cw) in enumerate(gcols):
                    def mmp(t_, gi=gi, c0=c0, cw=cw):
                        for k in range(8):
                            ins = t_.matmul(bank(1 + gi, cw), lhsT=hT[pi][:, k, :], rhs=w_in[:, k, c0:c0 + cw],
                                            start=(k == 0), stop=(k == 7))
                        return ins
                    P.op("pe", mmp, reads=[B_hT[pi], B_win], writes=[PB[1 + gi]])
                for gi in range(2):
                    P.op("act", lambda a, gi=gi: a.activation(out=uv[pi][:, gi * 512:(gi + 1) * 512], in_=bank(1 + gi),
                                                              func=AF.Gelu_apprx_tanh),
                         reads=[PB[1 + gi]], writes=[B_uv[pi]])
                qk_ps = ps[:, 3 * 512:3 * 512 + 640].rearrange("p (h d) -> p h d", d=64)
                cosb = cos_t[:, n, :].unsqueeze(1).to_broadcast([128, 10, 8])
                sinb = sin_t[:, n, :].unsqueeze(1).to_broadcast([128, 10, 8])
                t1 = qk_ps[:, :, 0:8]
                t2 = qk_ps[:, :, 8:16]
                PBqk = [PB[3], PB[4]]
                P.op("dve", lambda v: v.tensor_tensor(out=rt[0][:], in0=t1, in1=cosb, op=ALU.mult),
                     reads=PBqk + [B_cs], writes=[B_rt[0]])
                P.op("dve", lambda v: v.tensor_tensor(out=rt[1][:], in0=t2, in1=sinb, op=ALU.mult),
                     reads=PBqk + [B_cs], writes=[B_rt[1]])
                P.op("dve", lambda v: v.tensor_tensor(out=qkr[pi][:, :, 0:8], in0=rt[0][:], in1=rt[1][:], op=ALU.subtract),
                     reads=[B_rt[0], B_rt[1]], writes=[B_qkr[pi]])
                P.op("dve", lambda v: v.tensor_tensor(out=rt[2][:], in0=t2, in1=cosb, op=ALU.mult),
                     reads=PBqk + [B_cs], writes=[B_rt[2]])
                P.op("dve", lambda v: v.tensor_tensor(out=rt[3][:], in0=t1, in1=sinb, op=ALU.mult),
                     reads=PBqk + [B_cs], writes=[B_rt[3]])
                P.op("dve", lambda v: v.tensor_tensor(out=qkr[pi][:, :, 8:16], in0=rt[2][:], in1=rt[3][:], op=ALU.add),
                     reads=[B_rt[2], B_rt[3], B_qkr[pi]], writes=[B_qkr[pi]])
                P.op("act", lambda a: a.activation(out=qkr[pi][:, :, 16:64], in_=qk_ps[:, :, 16:64], func=AF.Copy),
                     reads=PBqk + [B_qkr[pi]], writes=[B_qkr[pi]])
                v_ps = bank(4, 128, 128).rearrange("p (h d) -> p h d", d=64)
                P.op("dve", lambda v: v.tensor_copy(out=Vaug[cur][:, :, 0:64], in_=v_ps),
                     reads=[PB[4], B_Vaug[cur]], writes=[B_Vaug[cur]])
                def msg(t_):
                    for g_ in range(8):
                        ins = t_.matmul(bank(5, 64, g_ * 64), lhsT=WsT[:, g_, :], rhs=uv[pi][:, 512 + g_ * 64:512 + (g_ + 1) * 64],
                                        start=True, stop=True)
                    return ins
                P.op("pe", msg, reads=[B_WsT, B_uv[pi]], writes=[PB[5]])
                P.op("dve", lambda v: v.tensor_tensor(out=tmp_sv[:].rearrange("p (g c) -> p g c", g=8),
                                                      in0=bank(5).rearrange("p (g c) -> p g c", g=8),
                                                      in1=bcol[:].unsqueeze(2).to_broadcast([128, 8, 64]), op=ALU.add),
                     reads=[PB[5], B_bcol], writes=[B_tmpsv])
                P.op("dve", lambda v: v.tensor_tensor(out=mix[pi][:, 0:512], in0=tmp_sv[:], in1=uv[pi][:, 0:512], op=ALU.mult),
                     reads=[B_tmpsv, B_uv[pi]], writes=[B_mix[pi]])
                b0q = bank_bf(0, 64).rearrange("p (h t) -> p h t", h=8)
                b4k = ps[0:64, 4 * 512 + 256:4 * 512 + 384].bitcast(BF16).rearrange("p (h t) -> p h t", h=2)

                def tq_(t_):
                    for h in range(8):
                        ins = t_.transpose(out=b0q[:, h, :], in_=qkr[pi][:, h, :], identity=identb[:])
                    return ins
                P.op("pe", tq_, reads=[B_qkr[pi], B_identb], writes=[PB[0]])

                def tk_(t_):
                    for h in range(2):
                        ins = t_.transpose(out=b4k[:, h, :], in_=qkr[pi][:, 8 + h, :], identity=identb[:])
                    return ins
                P.op("pe", tk_, reads=[B_qkr[pi], B_identb], writes=[PB[4]])
                P.op("act", lambda a: a.activation(out=QT[:], in_=b0q, func=AF.Copy), reads=[PB[0]], writes=[B_QT])
                P.op("dve", lambda v: v.tensor_copy(out=KT[cur][:], in_=b4k), reads=[PB[4]], writes=[B_KT[cur]])
                blks = ([(prv, 0)] if n > 0 else []) + [(cur, 1)]
                for hk in range(2):
                    for (bi, is_cur) in blks:
                        idx = hk * 2 + is_cur
                        P.op("pe", lambda t_, hk=hk, bi=bi, idx=idx: t_.matmul(
                            bank(1 + idx), lhsT=KT[bi][:, hk, :],
                            rhs=QT[:, 4 * hk:4 * hk + 4, :].rearrange("p h t -> p (h t)"), start=True, stop=True),
                            reads=[B_KT[bi], B_QT], writes=[PB[1 + idx]])
                        P.op("act", lambda a, idx=idx: a.activation(out=PT[idx][:], in_=bank(1 + idx), func=AF.Exp, scale=0.125),
                             reads=[PB[1 + idx]], writes=[B_PT[idx]])
                        mk = mask_cur if is_cur else mask_prev
                        Bm = B_mcur if is_cur else B_mprev
                        P.op("pool", lambda g, idx=idx, mk=mk: g.tensor_tensor(
                            out=PT[idx][:].rearrange("p (h t) -> p h t", h=4),
                            in0=PT[idx][:].rearrange("p (h t) -> p h t", h=4),
                            in1=mk[:].unsqueeze(1).to_broadcast([128, 4, 128]), op=ALU.mult),
                            reads=[B_PT[idx], Bm], writes=[B_PT[idx]])
                for hb in range(2):
                    def mpv(t_, hb=hb):
                        for g_ in range(4):
                            hk = hb
                            for bi_i, (bi, is_cur) in enumerate(blks):
                                idx = hk * 2 + is_cur
                                ins = t_.matmul(bank(6 + hb, 65, g_ * 65), lhsT=PT[idx][:, g_ * 128:(g_ + 1) * 128],
                                                rhs=Vaug[bi][:, hk, :], start=(bi_i == 0), stop=(bi_i == len(blks) - 1))
                        return ins
                    rd = [B_PT[hb * 2 + 1], B_Vaug[cur]] + ([B_PT[hb * 2], B_Vaug[prv]] if n > 0 else [])
                    P.op("pe", mpv, reads=rd, writes=[PB[6 + hb]])
                pv = ps[:, 6 * 512:8 * 512].rearrange("p (b c) -> p b c", b=2)[:, :, 0:260].rearrange("p b (h e) -> p b h e", e=65)
                P.op("dve", lambda v: v.tensor_tensor(out=den[:].rearrange("p (b h e) -> p b h e", b=2, e=1),
                                                      in0=pv[:, :, :, 64:65],
                                                      in1=esink[:].rearrange("p (b h e) -> p b h e", b=2, e=1), op=ALU.add),
                     reads=[PB[6], PB[7], B_esink], writes=[B_den])
                P.op("dve", lambda v: v.reciprocal(out=rden[:], in_=den[:]), reads=[B_den], writes=[B_rden])
                P.op("dve", lambda v: v.tensor_tensor(out=mix[pi][:, 512:1024].rearrange("p (b h d) -> p b h d", b=2, d=64),
                                                      in0=pv[:, :, :, 0:64],
                                                      in1=rden[:].rearrange("p (b h e) -> p b h e", b=2, e=1).to_broadcast([128, 2, 4, 64]),
                                                      op=ALU.mult),
                     reads=[PB[6], PB[7], B_rden, B_mix[pi]], writes=[B_mix[pi]])
                def tmx(t_):
                    for c_ in range(8):
                        ins = t_.transpose(out=b0[:, c_, :], in_=mix[pi][:, c_ * 128:(c_ + 1) * 128], identity=identb[:])
                    return ins
                P.op("pe", tmx, reads=[B_mix[pi], B_identb], writes=[PB[0]])
                P.op("act", lambda a: a.activation(out=mixT[pi][:].rearrange("p c t -> p (c t)"), in_=bank_bf(0), func=AF.Copy),
                     reads=[PB[0]], writes=[B_mixT[pi]])
                for half in range(2):
                    def mo(t_, half=half):
                        for c_ in range(8):
                            ins = t_.matmul(bank(1 + half), lhsT=mixT[pi][:, c_, :], rhs=w_out[:, c_, half * 512:(half + 1) * 512],
                                            start=(c_ == 0), stop=(c_ == 7))
                        return ins
                    P.op("pe", mo, reads=[B_mixT[pi], B_wout], writes=[PB[1 + half]])
                    P.op("dve", lambda v, half=half: v.tensor_tensor(out=xb[xi][:, half * 512:(half + 1) * 512], in0=bank(1 + half),
                                                                      in1=xb[xi][:, half * 512:(half + 1) * 512], op=ALU.add),
                         reads=[PB[1 + half], B_xb[xi]], writes=[B_xb[xi]])
                P.dma("sp", ds_x1[xi], x1_d[n * 128:(n + 1) * 128, :], xb[xi][:], reads=[B_xb[xi]])

            P.barrier()

        with ExitStack() as es2:
            sb2 = lambda name, shape, dtype: es2.enter_context(nc.sbuf_tensor(name, shape, dtype))
            w_ff2 = sb2("w_ff2b", [128, 32, D], BF16)
            h2T = sb2("h2T", [128, 8, 512], BF16)
            aT = sb2("aT", [128, 32, 512], BF16)
            xq = [sb2("xq%d" % i, [128, D], F32) for i in range(2)]
            xr = [sb2("xr%d" % i, [128, D], F32) for i in range(2)]
            xn2 = [sb2("xn2_%d" % i, [128, D], BF16) for i in range(4)]
            rl = [sb2("rl%d" % i, [128, 512], F32) for i in range(2)]
            B_wff2, B_h2T = Buf("wff2"), Buf("h2T")
            junk2_ap, B_junk2 = ps[:, 6 * 512:8 * 512], PB[7]
            B_aT = [Buf("aT%d" % f) for f in range(32)]
            B_xq = [Buf("xq%d" % i) for i in range(2)]
            B_xr = [Buf("xr%d" % i) for i in range(2)]
            B_xn2 = [Buf("xn2_%d" % i) for i in range(4)]
            B_rl = [Buf("rl%d" % i) for i in range(2)]
            ds_xq = [P.dma_sem() for _ in range(2)]
            ds_xr = [P.dma_sem() for _ in range(2)]
            ds_y = [P.dma_sem() for _ in range(2)]

            def norm_front(n, j):
                qi = n % 2
                P.dma("sp", ds_xq[qi], xq[qi][:], x1_d[n * 128:(n + 1) * 128, :], writes=[B_xq[qi]])
                slot = n % 2
                rms_stats(xq[qi][:], B_xq[qi], slot, junk2_ap, B_junk2)
                P.op("pool", lambda g: g.tensor_scalar(out=xn2[j][:], in0=xq[qi][:], scalar1=rstd[:, slot:slot + 1],
                                                       scalar2=1.0, op0=ALU.mult, op1=ALU.mult),
                     reads=[B_xq[qi], B_rstd[slot]], writes=[B_xn2[j]])

            def norm_back(n, j):
                qi = n % 2
                b0 = bank_bf(0).rearrange("p (k t) -> p k t", k=8)

                def tx(t_):
                    for k in range(8):
                        ins = t_.transpose(out=b0[:, k, :], in_=xn2[j][:, k * 128:(k + 1) * 128], identity=identb[:])
                    return ins
                P.op("pe", tx, reads=[B_xn2[j], B_identb], writes=[PB[0]])
                for k in range(8):
                    P.op("act", lambda a, k=k: a.activation(out=h2T[:, k, j * 128:(j + 1) * 128], in_=b0[:, k, :], func=AF.Identity,
                                                            scale=Gc[:, 8 + k:9 + k], bias=modcol[:, 16 + k:17 + k]),
                         reads=[PB[0], B_Gc, B_modcol], writes=[B_h2T])

            for j in range(4):
                norm_front(j, j)
                norm_back(j, j)
            wff2_v = wff2_d.rearrange("(f p) n -> p f n", p=128)
            for f in range(32):
                ri = f % 2
                P.dma("sp", ds_xr[ri], xr[ri][:], wff2_v[:, f, :], writes=[B_xr[ri]])
                P.op("pool", lambda g, f=f, ri=ri: g.tensor_tensor(out=w_ff2[:, f, :], in0=xr[ri][:], in1=gate2_bc[:], op=ALU.mult),
                     reads=[B_xr[ri], B_gate2], writes=[B_wff2])

            out_toks = []
            for s in range(NS):
                for f in range(32):
                    bk = 1 + f % 3
                    def m1(t_, f=f, bk=bk):
                        for k in range(8):
                            ins = t_.matmul(bank(bk), lhsT=w_ff1[:, k, f * 128:(f + 1) * 128], rhs=h2T[:, k, :],
                                            start=(k == 0), stop=(k == 7))
                        return ins
                    P.op("pe", m1, reads=[B_wff1, B_h2T], writes=[PB[bk]])
                    ri = f % 2
                    P.op("act", lambda a, bk=bk, ri=ri: a.activation(out=rl[ri][:], in_=bank(bk), func=AF.Relu),
                         reads=[PB[bk]], writes=[B_rl[ri]])
                    P.op("dve", lambda v, f=f, bk=bk, ri=ri: v.tensor_tensor(out=aT[:, f, :], in0=bank(bk), in1=rl[ri][:], op=ALU.mult),
                         reads=[PB[bk], B_rl[ri]], writes=[B_aT[f]])
                    if s + 1 < NS and f % 8 == 4:
                        jj = f // 8
                        norm_front(4 * (s + 1) + jj, jj)
                if s + 1 < NS:
                    for jj in range(4):
                        norm_back(4 * (s + 1) + jj, jj)
                for j in range(4):
                    n = 4 * s + j
                    ri = n % 2
                    P.dma("sp", ds_xr[ri], xr[ri][:], x1_d[n * 128:(n + 1) * 128, :], writes=[B_xr[ri]])
                    for half in range(2):
                        bk = 4 + (2 * j + half) % 2
                        def m2(t_, j=j, half=half, bk=bk):
                            for f in range(32):
                                ins = t_.matmul(bank(bk), lhsT=aT[:, f, j * 128:(j + 1) * 128],
                                                rhs=w_ff2[:, f, half * 512:(half + 1) * 512], start=(f == 0), stop=(f == 31))
                            return ins
                        P.op("pe", m2, reads=B_aT + [B_wff2], writes=[PB[bk]])
                        P.op("dve", lambda v, half=half, bk=bk, ri=ri: v.tensor_tensor(
                            out=xr[ri][:, half * 512:(half + 1) * 512], in0=bank(bk),
                            in1=xr[ri][:, half * 512:(half + 1) * 512], op=ALU.add),
                            reads=[PB[bk], B_xr[ri]], writes=[B_xr[ri]])
                    slot = 2 + n % 2
                    rms_stats(xr[ri][:], B_xr[ri], slot, junk2_ap, B_junk2)
                    P.op("dve", lambda v, ri=ri, slot=slot: v.scalar_tensor_tensor(out=xr[ri][:], in0=xr[ri][:],
                                                                                   scalar=rstd[:, slot:slot + 1], in1=gfin_bc[:],
                                                                                   op0=ALU.mult, op1=ALU.mult),
                         reads=[B_xr[ri], B_rstd[slot], B_gfin], writes=[B_xr[ri]])
                    out_toks.append(P.dma("sp", ds_y[ri], y_d[n * 128:(n + 1) * 128, :], xr[ri][:], reads=[B_xr[ri]]))
            P.finish(out_toks)
    return nc


_NC_CACHE = {}


def _run(inputs, NT):
    S = NT * 128
    if NT not in _NC_CACHE:
        _NC_CACHE[NT] = _build(NT)
    nc = _NC_CACHE[NT]
    f32 = lambda a: np.ascontiguousarray(np.asarray(a, dtype=np.float32))
    x = np.asarray(inputs["x"])
    c = np.asarray(inputs["c"])
    pos = np.asarray(inputs["positions"])
    shared = {
        "w_ada": f32(inputs["w_ada"][0]),
        "b_ada": f32(inputs["b_ada"]).reshape(1, NMOD * D),
        "g_mix": f32(inputs["g_mix"]).reshape(8, 128),
        "w_in": f32(inputs["w_in"][0]),
        "w_spatial": f32(inputs["w_spatial"][0]),
        "b_spatial": f32(inputs["b_spatial"][0]),
        "sinks": f32(inputs["sinks"]).reshape(8),
        "w_out": f32(inputs["w_out"][0]),
        "g_ffn": f32(inputs["g_ffn"]).reshape(8, 128),
        "w_ff1": f32(inputs["w_ff1"][0]),
        "w_ff2": f32(inputs["w_ff2"][0]),
        "g_final": f32(inputs["g_final"]).reshape(D),
    }
    in_maps = []
    for b in range(N_CORES):
        m = dict(shared)
        m["x"] = f32(x[b, :S])
        m["c"] = f32(c[b]).reshape(8, 128)
        m["positions"] = np.ascontiguousarray(pos[b, :S].astype(np.int32)).reshape(NT, 128)
        in_maps.append(m)
    res = run_bass_kernel_spmd(nc, in_maps, core_ids=list(range(N_CORES)))
    return np.stack([np.asarray(r["y"]) for r in res.results], axis=0).astype(np.float32)


def kernel(x, c, positions, w_ada, b_ada, g_mix, w_in, w_spatial, b_spatial, sinks,
           w_out, g_ffn, w_ff1, w_ff2, g_final):
    inputs = dict(x=x, c=c, positions=positions, w_ada=w_ada, b_ada=b_ada, g_mix=g_mix, w_in=w_in,
                  w_spatial=w_spatial, b_spatial=b_spatial, sinks=sinks, w_out=w_out, g_ffn=g_ffn,
                  w_ff1=w_ff1, w_ff2=w_ff2, g_final=g_final)
    return _run(inputs, 32)
```
